# Optimizing a Trainium2 kernel written in Bass

```python
import jax, jax.numpy as jnp
from jax import lax
import numpy as np

D_MODEL = 1024
BATCH = 8
SEQ = 8192
DEPTH = 4

CHUNK = 128
N_MIXERS = 2
N_SGU_LAYERS = (DEPTH + 1) // 2
N_RET_LAYERS = DEPTH // 2
EPS = 1e-6
SGU_WIDTH = D_MODEL
SGU_GROUPS = 8
SGU_GROUP_DIM = SGU_WIDTH // SGU_GROUPS
RET_HEADS = 4
RET_QK_DIM = D_MODEL // RET_HEADS
RET_V_DIM = 2 * D_MODEL // RET_HEADS
RET_QK_WIDTH = RET_HEADS * RET_QK_DIM
RET_V_WIDTH = RET_HEADS * RET_V_DIM
RET_IN_WIDTH = 2 * RET_QK_WIDTH + 2 * RET_V_WIDTH
ROPE_BASE = 10000.0
FFN_HIDDEN = -(-8 * D_MODEL // (3 * 256)) * 256

kernel_name = "hybrid_sgu_retention_trunk"


def rmsnorm(x, g):
    xf = x.astype(jnp.float32)
    y = xf * lax.rsqrt(jnp.mean(xf * xf, axis=-1, keepdims=True) + EPS)
    return (y * g.astype(jnp.float32)).astype(x.dtype)


def layernorm(x, g, b):
    xf = x.astype(jnp.float32)
    mu = jnp.mean(xf, axis=-1, keepdims=True)
    var = jnp.mean(jnp.square(xf - mu), axis=-1, keepdims=True)
    y = (xf - mu) * lax.rsqrt(var + EPS)
    return (y * g.astype(jnp.float32) + b.astype(jnp.float32)).astype(x.dtype)


def sgu_mixer(h, w_in, w_s, b_s, ln_g, ln_b, w_out):
    B, S, _ = h.shape
    nc = S // CHUNK
    z = jax.nn.gelu(h @ w_in)
    u, v = jnp.split(z, 2, axis=-1)
    v = layernorm(v, ln_g, ln_b)
    v = v.reshape(B, nc, CHUNK, SGU_GROUPS, SGU_GROUP_DIM)
    causal = jnp.tril(jnp.ones((CHUNK, CHUNK), dtype=bool))
    w = jnp.where(causal[None], w_s, jnp.zeros_like(w_s))
    mixed = jnp.einsum('gts,bcsgd->bctgd', w, v)
    mixed = mixed + b_s.T[None, None, :, :, None]
    mixed = mixed.reshape(B, S, SGU_WIDTH)
    return (u * mixed) @ w_out


def rotary(x, positions):
    half = x.shape[-1] // 2
    inv = 1.0 / (ROPE_BASE ** (jnp.arange(half, dtype=jnp.float32) / half))
    ang = positions.astype(jnp.float32)[..., None] * inv
    cos = jnp.cos(ang)[:, :, None, :]
    sin = jnp.sin(ang)[:, :, None, :]
    xf = x.astype(jnp.float32)
    x1, x2 = xf[..., :half], xf[..., half:]
    return jnp.concatenate([x1 * cos - x2 * sin, x1 * sin + x2 * cos], axis=-1)


def retention(h, positions, w_in, gn_g, w_out):
    B, S, _ = h.shape
    nc = S // CHUNK
    proj = h @ w_in
    q, k, v, g = jnp.split(proj, [RET_QK_WIDTH, 2 * RET_QK_WIDTH,
                                  2 * RET_QK_WIDTH + RET_V_WIDTH], axis=-1)
    q = rotary(q.reshape(B, S, RET_HEADS, RET_QK_DIM), positions)
    k = rotary(k.reshape(B, S, RET_HEADS, RET_QK_DIM), positions) * (RET_QK_DIM ** -0.5)
    v = v.reshape(B, S, RET_HEADS, RET_V_DIM).astype(jnp.float32)

    def to_chunks(t):
        return t.reshape(B, nc, CHUNK, RET_HEADS, -1).transpose(1, 0, 3, 2, 4)

    log_gamma = jnp.log1p(-(2.0 ** (-5.0 - jnp.arange(RET_HEADS, dtype=jnp.float32))))
    idx = jnp.arange(CHUNK, dtype=jnp.float32)
    diff = idx[:, None] - idx[None, :]
    decay_inner = jnp.where(diff >= 0,
                            jnp.exp(log_gamma[:, None, None] * jnp.maximum(diff, 0.0)),
                            0.0)
    cross_decay = jnp.exp(log_gamma[:, None] * (idx + 1.0))
    state_decay = jnp.exp(log_gamma[:, None] * (CHUNK - 1.0 - idx))
    chunk_decay = jnp.exp(log_gamma * CHUNK)

    def step(state, qkv):
        qc, kc, vc = qkv
        scores = jnp.einsum('bhtd,bhsd->bhts', qc, kc) * decay_inner[None]
        inner = jnp.einsum('bhts,bhsv->bhtv', scores, vc)
        cross = jnp.einsum('bhtd,bhdv->bhtv', qc, state) * cross_decay[None, :, :, None]
        new_state = state * chunk_decay[None, :, None, None] + jnp.einsum(
            'bhsd,bhsv->bhdv', kc * state_decay[None, :, :, None], vc)
        return new_state, inner + cross

    state0 = jnp.zeros((B, RET_HEADS, RET_QK_DIM, RET_V_DIM), jnp.float32)
    _, o = lax.scan(step, state0, (to_chunks(q), to_chunks(k), to_chunks(v)))
    o = o.transpose(1, 0, 3, 2, 4).reshape(B, S, RET_HEADS, RET_V_DIM)
    o = o * lax.rsqrt(jnp.mean(o * o, axis=-1, keepdims=True) + EPS)
    o = o.reshape(B, S, RET_V_WIDTH) * gn_g.astype(jnp.float32)
    return (o.astype(h.dtype) * jax.nn.silu(g)) @ w_out


def swiglu(h, w_gu, w_down):
    a, b = jnp.split(h @ w_gu, 2, axis=-1)
    return (jax.nn.silu(a) * b) @ w_down


def setup_inputs(seed: int = 0) -> dict:
    key = jax.random.key(seed)
    ks = jax.random.split(key, 16)
    f32 = jnp.float32

    def nrm(k, shape, scale):
        return jax.random.normal(k, shape, f32) * scale

    na, nb = N_SGU_LAYERS, N_RET_LAYERS
    x = jax.random.normal(ks[0], (BATCH, SEQ, D_MODEL), f32)
    positions = jnp.broadcast_to(jnp.arange(SEQ, dtype=jnp.int32), (BATCH, SEQ))
    return {
        "x": x,
        "positions": positions,
        "mix_norm_g": 1.0 + nrm(ks[1], (DEPTH, D_MODEL), 0.02),
        "ffn_norm_g": 1.0 + nrm(ks[2], (DEPTH, D_MODEL), 0.02),
        "final_norm_g": 1.0 + nrm(ks[3], (D_MODEL,), 0.02),
        "sgu_w_in": nrm(ks[4], (na, D_MODEL, 2 * SGU_WIDTH), D_MODEL ** -0.5),
        "sgu_w_s": nrm(ks[5], (na, SGU_GROUPS, CHUNK, CHUNK), CHUNK ** -0.5),
        "sgu_b_s": 1.0 + nrm(ks[6], (na, SGU_GROUPS, CHUNK), 0.02),
        "sgu_ln_g": 1.0 + nrm(ks[7], (na, SGU_WIDTH), 0.02),
        "sgu_ln_b": nrm(ks[8], (na, SGU_WIDTH), 0.02),
        "sgu_w_out": nrm(ks[9], (na, SGU_WIDTH, D_MODEL), SGU_WIDTH ** -0.5),
        "ret_w_in": nrm(ks[10], (nb, D_MODEL, RET_IN_WIDTH), D_MODEL ** -0.5),
        "ret_gn_g": 1.0 + nrm(ks[11], (nb, RET_V_WIDTH), 0.02),
        "ret_w_out": nrm(ks[12], (nb, RET_V_WIDTH, D_MODEL), RET_V_WIDTH ** -0.5),
        "ffn_w_gu": nrm(ks[13], (DEPTH, D_MODEL, 2 * FFN_HIDDEN), D_MODEL ** -0.5),
        "ffn_w_down": nrm(ks[14], (DEPTH, FFN_HIDDEN, D_MODEL), FFN_HIDDEN ** -0.5),
    }


def reference(x, positions, mix_norm_g, ffn_norm_g, final_norm_g,
              sgu_w_in, sgu_w_s, sgu_b_s, sgu_ln_g, sgu_ln_b, sgu_w_out,
              ret_w_in, ret_gn_g, ret_w_out, ffn_w_gu, ffn_w_down):
    h = x
    for i in range(DEPTH):
        hn = rmsnorm(h, mix_norm_g[i])
        j = i // N_MIXERS
        if i % N_MIXERS == 0:
            h = h + sgu_mixer(hn, sgu_w_in[j], sgu_w_s[j], sgu_b_s[j],
                              sgu_ln_g[j], sgu_ln_b[j], sgu_w_out[j])
        else:
            h = h + retention(hn, positions, ret_w_in[j], ret_gn_g[j], ret_w_out[j])
        h = h + swiglu(rmsnorm(h, ffn_norm_g[i]), ffn_w_gu[i], ffn_w_down[i])
    return rmsnorm(h, final_norm_g)
```

```python
import math
import numpy as np
import concourse.bass as bass
import concourse.mybir as mybir
from concourse.bass_utils import run_bass_kernel_spmd

F32 = mybir.dt.float32
BF16 = mybir.dt.bfloat16
I32 = mybir.dt.int32
AF = mybir.ActivationFunctionType
ALU = mybir.AluOpType

D = 1024
S = 8192
DEPTH = 4
T = 512
NCH = 4
KC = 8
FFH = 2816
FFC = 22
NHEAD = 4
EPS = 1e-6
NSLOT = 6
SLOT_EL = 4096
ENGS = ["sync", "scalar", "vector", "gpsimd", "tensor"]

C_ID = 0
C_M01 = 128
C_MT = 256
C_CD = 768
C_SD = 1280
C_INV = 1284
NCST = 1288

def G_MIX(l, kc): return l * 8 + kc
def G_FFN(l, kc): return 32 + l * 8 + kc
def G_FIN(kc): return 64 + kc
def G_LNG(j, g): return 72 + j * 8 + g
def G_LNB(j, g): return 88 + j * 8 + g
def G_GN(j, fc): return 104 + j * 16 + fc


class Prog:
    def __init__(self, nc):
        self.nc = nc
        self.ops = {e: [] for e in ENGS}
        self.esem = {e: nc.alloc_semaphore("c_" + e) for e in ENGS}
        self.ecnt = {e: 0 for e in ENGS}
        self.pend = {e: False for e in ENGS}
        self.seen = {e: {} for e in ENGS}
        self.dsem = {}
        self.dcnt = {}
        self.lastw = {}
        self.readers = {}
        self.nops = 0
        self.nwaits = 0

    def _deps(self, eng, reads, writes, skip_waw=False):
        best = {}

        def add(tok):
            s, v, src = tok
            if src == "tensor" and eng == "tensor":
                return
            k = id(s)
            if k not in best or best[k][1] < v:
                best[k] = (s, v)

        for b in reads:
            w = self.lastw.get(b)
            if w is not None:
                add(w)
        for b in writes:
            w = self.lastw.get(b)
            if w is not None and not (skip_waw and w[2] == eng):
                add(w)
            for r in self.readers.get(b, ()):
                add(r)
        out = []
        seen = self.seen[eng]
        for k, (s, v) in best.items():
            if seen.get(k, 0) >= v:
                continue
            seen[k] = v
            out.append((s, v))
        self.nwaits += len(out)
        return out

    def _record(self, tok, reads, writes):
        for b in writes:
            self.lastw[b] = tok
            self.readers[b] = []
        ws = set(writes)
        for b in reads:
            if b in ws:
                continue
            lst = self.readers.setdefault(b, [])
            for i, (s, v, src) in enumerate(lst):
                if s is tok[0]:
                    if v < tok[1]:
                        lst[i] = tok
                    break
            else:
                lst.append(tok)

    def op(self, eng, meth, kw, reads=(), writes=(), inc=True, skip_waw=False):
        waits = self._deps(eng, reads, writes, skip_waw)
        if inc:
            self.ecnt[eng] += 1
            self.pend[eng] = False
            val = self.ecnt[eng]
            incv = (self.esem[eng], 1)
        else:
            val = self.ecnt[eng] + 1
            self.pend[eng] = True
            incv = None
        tok = (self.esem[eng], val, eng)
        self.ops[eng].append((waits, meth, kw, incv))
        self._record(tok, reads, writes)
        self.nops += 1
        return tok

    def dma(self, eng, key, kw, reads=(), writes=()):
        waits = self._deps(eng, reads, writes)
        if key not in self.dsem:
            self.dsem[key] = self.nc.alloc_semaphore("d_" + key)
            self.dcnt[key] = 0
        self.dcnt[key] += 16
        tok = (self.dsem[key], self.dcnt[key], "dma")
        self.ops[eng].append((waits, "dma_start", kw, (self.dsem[key], 16)))
        self._record(tok, reads, writes)
        self.nops += 1
        return tok

    def final_wait(self, eng, toks):
        self.ops[eng].append(([(s, v) for (s, v, _) in toks], None, None, None))

    def finish(self):
        for e in ENGS:
            assert not self.pend[e], e
        with self.nc.Block() as block:
            for en in ENGS:
                ops = self.ops[en]
                if not ops:
                    continue

                def body(e, ops=ops):
                    for waits, meth, kw, incv in ops:
                        for (s, v) in waits:
                            e.wait_ge(s, v)
                        if meth is None:
                            continue
                        ins = getattr(e, meth)(**kw)
                        if incv is not None:
                            ins.then_inc(incv[0], incv[1])

                getattr(block, en)(body)


class Buf:
    def __init__(self, name, t, parts):
        self.name = name
        self.t = t
        self.parts = parts
        self._all = [a for p in parts for a in p]

    def A(self, i=None):
        if i is None:
            return self._all
        return self.parts[i]


def build_program(NT=S // T, nlayers=DEPTH, dbg=None):
    nc = bass.Bass("TRN2", target_bir_lowering=False)
    P = Prog(nc)

    def din(name, shape, dt=F32):
        return nc.dram_tensor(name, shape, dt, kind="ExternalInput")

    x_d = din("x", [S, D])
    pos_d = din("positions", [S], I32)
    mixg_d = din("mix_norm_g", [4, D])
    ffng_d = din("ffn_norm_g", [4, D])
    fing_d = din("final_norm_g", [D])
    sgu_w_in_d = din("sgu_w_in", [2, D, 2 * D])
    sgu_w_s_d = din("sgu_w_s", [2, 8, 128, 128])
    sgu_b_s_d = din("sgu_b_s", [2, 8, 128])
    ln_g_d = din("sgu_ln_g", [2, D])
    ln_b_d = din("sgu_ln_b", [2, D])
    sgu_w_out_d = din("sgu_w_out", [2, D, D])
    ret_w_in_d = din("ret_w_in", [2, D, 6 * D])
    gn_g_d = din("ret_gn_g", [2, 2 * D])
    ret_w_out_d = din("ret_w_out", [2, 2 * D, D])
    w_gu_d = din("ffn_w_gu", [4, D, 2 * FFH])
    w_down_d = din("ffn_w_down", [4, FFH, D])
    cst_d = din("cst", [128, NCST])
    out_d = nc.dram_tensor("out", [S, D], F32, kind="ExternalOutput")

    pieces = {}
    plist = []

    def add_piece(key, K, cols, srcs, fold):
        pieces[key] = dict(idx=len(plist), K=K, cols=cols, srcs=srcs, fold=fold, key=key)
        plist.append(pieces[key])

    def wsrc(dt, idx, c0, c1):
        return dt.ap()[idx].rearrange("(kc p) c -> p kc c", p=128)[:, :, c0:c1]

    for l in range(DEPTH):
        j = l // 2
        if l % 2 == 0:
            for p in range(4):
                add_piece(("si", l, p), 8, 512, [(wsrc(sgu_w_in_d, j, p * 512, (p + 1) * 512), 0)],
                          [G_MIX(l, kc) for kc in range(8)])
            for p in range(2):
                add_piece(("so", l, p), 8, 512, [(wsrc(sgu_w_out_d, j, p * 512, (p + 1) * 512), 0)], None)
        else:
            for p in range(12):
                add_piece(("ri", l, p), 8, 512, [(wsrc(ret_w_in_d, j, p * 512, (p + 1) * 512), 0)],
                          [G_MIX(l, kc) for kc in range(8)])
            for p in range(4):
                add_piece(("ro", l, p), 16, 256, [(wsrc(ret_w_out_d, j, p * 256, (p + 1) * 256), 0)],
                          [G_GN(j, kc) for kc in range(16)])
        for p in range(11):
            add_piece(("fi", l, p), 8, 512,
                      [(wsrc(w_gu_d, l, p * 256, (p + 1) * 256), 0),
                       (wsrc(w_gu_d, l, FFH + p * 256, FFH + (p + 1) * 256), 256)],
                      [G_FFN(l, kc) for kc in range(8)])
        for p in range(8):
            add_piece(("fo", l, p), 22, 128, [(wsrc(w_down_d, l, p * 128, (p + 1) * 128), 0)], None)
    NP = len(plist)
    scr_d = nc.dram_tensor("wscr", [NP, 128, SLOT_EL], BF16, kind="Internal")

    def sb(name, shape, dt, nparts=1):
        t = nc.alloc_sbuf_tensor(name, shape, dt)
        return Buf(name, t, [["%s.%d" % (name, i)] for i in range(nparts)])

    hT = sb("hT", [128, KC, T], F32, KC)
    stf = [sb("stf%d" % j, [128, 2, NHEAD, 512], F32, 8) for j in range(2)]
    stb = [sb("stb%d" % j, [128, 2, NHEAD, 512], BF16, 8) for j in range(2)]
    ring = [sb("ring%d" % i, [128, SLOT_EL], BF16) for i in range(NSLOT)]
    cst = sb("cstb", [128, NCST], F32)
    identb = sb("identb", [128, 128], BF16)
    onesb = sb("onesb", [128, 128], BF16)
    gv = sb("gv", [128, 136], F32)
    wTb = sb("wTb", [128, 2, 8, 128], BF16)
    Cc = sb("Cc", [128, 2, 8, 128], F32)
    cosT = sb("cosT", [128, T], F32)
    sinT = sb("sinT", [128, T], F32)
    st6 = sb("st6", [128, NCH, 2, 6], F32, NCH)
    mv = sb("mv", [128, NCH, 2], F32, NCH)
    lnr = sb("lnr", [128, NCH], F32, NCH)
    ss4 = sb("ss4", [128, 2, NHEAD], F32, 2)
    rs4 = sb("rs4", [128, 2, NHEAD], F32, 2)
    dmy = sb("dmy", [128, 2], F32)

    ARENA = 64 * 1024
    arena_b = nc.alloc_sbuf_tensor("arena", [128, ARENA // 2], BF16)
    arena_f = arena_b.bitcast(F32)
    arena_i = arena_b.bitcast(I32)

    def ar(name, off_kib, shape, dt, nparts=1):
        es = 2 if dt == BF16 else 4
        n = 1
        for s_ in shape[1:]:
            n *= s_
        off = int(off_kib * 1024)
        nbytes = n * es
        assert off + nbytes <= ARENA, name
        base = {BF16: arena_b, F32: arena_f, I32: arena_i}[dt]
        ap = base[:, off // es: off // es + n]
        if len(shape) == 3:
            ap = ap.rearrange("p (a b) -> p a b", a=shape[1])
        elif len(shape) == 4:
            ap = ap.rearrange("p (a b c) -> p a b c", a=shape[1], b=shape[2])
        psz = nbytes // nparts
        parts = []
        for i in range(nparts):
            b0 = (off + i * psz) // 1024
            b1 = (off + (i + 1) * psz - 1) // 1024
            parts.append(["A%d" % b for b in range(b0, b1 + 1)])
        return Buf(name, ap, parts)

    hn = ar("hn", 0, [128, KC, T], BF16, KC)
    sq = ar("sq", 8, [128, KC, T], BF16, KC)
    uT = ar("uT", 8, [128, 8, T], F32, 8)
    vtm = ar("vtm", 24, [128, NCH, D], F32, NCH)
    vhat = ar("vhat", 40, [128, NCH, D], BF16, NCH)
    sgT = ar("sgT", 48, [128, 8, T], BF16, 8)
    stmp = [ar("stmp%d" % i, 56 + 2 * i, [128, T], F32) for i in range(2)]
    qcd = ar("qcd", 8, [128, 8, T], BF16, 8)
    krot = ar("krot", 16, [128, 8, T], BF16, 8)
    vret = ar("vret", 24, [128, NCH, 2 * D], BF16, NCH * 4)
    ksd = [ar("ksd%d" % i, 40 + 2 * i, [128, D], BF16) for i in range(2)]
    sT = [ar("sT%d" % i, 44 + i, [128, 512], BF16) for i in range(2)]
    on = ar("on", 46, [128, NCH, 2 * D], BF16, NCH * 4)
    rtmp = [[ar("rt%d_%d" % (s_, i), 46 + 8 * s_ + 2 * i, [128, T], F32) for i in range(4)] for s_ in range(2)]
    qtmp = [ar("qtmp%d" % i, 40 + 2 * i, [128, T], F32) for i in range(2)]
    sgtmp = [ar("sgtmp%d" % i, 40 + 2 * i, [128, T], F32) for i in range(2)]
    junk = [ar("junk%d" % i, 62 + i, [128, T], BF16) for i in range(2)]
    grT = ar("grT", 24, [128, 16, T], BF16, 16)
    hff = ar("hff", 8, [128, FFC, T], BF16, FFC)
    satmp = [ar("satmp%d" % i, 30 + 2 * i, [128, T], F32) for i in range(2)]
    fin = ar("fin", 8, [128, KC, T], F32, KC)
    ostage = ar("ostage", 24, [128, NCH, D], F32, NCH)
    xin = ar("xin", 40, [128, NCH, D], F32, NCH)
    cvf = [ar("cvf%d" % i, 16 * i, [128, SLOT_EL], F32) for i in range(2)]
    cvb = [ar("cvb%d" % i, 32 + 8 * i, [128, SLOT_EL], BF16) for i in range(2)]
    vecst = ar("vecst", 48, [128, 128], F32)
    vecst2 = ar("vecst2", 49, [128, 128], F32)
    wsl = [ar("wsl%d" % i, 50 + i, [128, 128], F32) for i in range(2)]
    wtm = [ar("wtm%d" % i, 52 + i, [128, 128], F32) for i in range(2)]
    bsb = [ar("bsb%d" % i, 54 + i, [128, 128], F32) for i in range(2)]
    onesf = ar("onesf", 56, [128, 128], F32)
    posi = ar("posi", 56, [128, T], I32)
    posf = ar("posf", 58, [128, T], F32)
    angb = ar("angb", 60, [128, T], F32)
    kfb = ar("kfb", 62, [128, T], F32)
    kib = ar("kib", 56, [128, T], I32)

    ps = [nc.alloc_psum_tensor("ps%d" % i, [128, 512], F32) for i in range(8)]
    psb = [p_.bitcast(BF16) for p_ in ps]
    PSA = [["ps%d" % i] for i in range(8)]
    bank_ctr = [0]

    def nb():
        b = bank_ctr[0] % 8
        bank_ctr[0] += 1
        return b

    flip = [0]

    def evac_eng():
        flip[0] ^= 1
        return "scalar" if flip[0] else "vector"

    def mm(out, lhsT, rhs, start, stop, reads, writes, inc):
        return P.op("tensor", "matmul", dict(out=out, lhsT=lhsT, rhs=rhs, start=start, stop=stop),
                    reads=reads, writes=writes, inc=inc)

    def tr(out, in_, ident, reads, writes, inc):
        return P.op("tensor", "transpose", dict(out=out, in_=in_, identity=ident),
                    reads=reads, writes=writes, inc=inc)

    def act(out, in_, func, reads, writes, **kw):
        return P.op("scalar", "activation", dict(out=out, in_=in_, func=func, **kw), reads=reads, writes=writes)

    def tt(eng, out, in0, in1, op, reads, writes, skip_waw=False):
        return P.op(eng, "tensor_tensor", dict(out=out, in0=in0, in1=in1, op=op), reads=reads, writes=writes,
                    skip_waw=skip_waw)

    def ts(eng, out, in0, s1, s2, op0, op1, reads, writes, skip_waw=False):
        kw = dict(out=out, in0=in0, scalar1=s1, scalar2=s2, op0=op0)
        if op1 is not None:
            kw["op1"] = op1
        return P.op(eng, "tensor_scalar", kw, reads=reads, writes=writes, skip_waw=skip_waw)

    def stt(out, in0, scalar, in1, op0, op1, reads, writes, skip_waw=False):
        return P.op("vector", "scalar_tensor_tensor",
                    dict(out=out, in0=in0, scalar=scalar, in1=in1, op0=op0, op1=op1), reads=reads, writes=writes,
                    skip_waw=skip_waw)

    def cp(eng, out, in_, reads, writes):
        if eng == "scalar":
            return act(out, in_, AF.Copy, reads, writes)
        return P.op(eng, "tensor_copy", dict(out=out, in_=in_), reads=reads, writes=writes)

    def gcol(c):
        return gv.t[:, c:c + 1]

    ident = cst.t[:, C_ID:C_ID + 128]

    P.dma("sync", "cst", dict(out=cst.t[:], in_=cst_d.ap()), writes=cst.A())
    P.op("vector", "tensor_copy", dict(out=identb.t[:], in_=ident), reads=cst.A(), writes=identb.A())
    P.op("vector", "memset", dict(ap=onesb.t[:], constant=1.0), writes=onesb.A())
    P.op("vector", "memset", dict(ap=onesf.t[:], constant=1.0), writes=onesf.A())
    P.op("vector", "memset", dict(ap=dmy.t[:], constant=1.0), writes=["dmy.in"])
    for j in range(2):
        P.op("gpsimd", "memset", dict(ap=stf[j].t[:], constant=0.0), writes=stf[j].A())
        P.op("gpsimd", "memset", dict(ap=stb[j].t[:], constant=0.0), writes=stb[j].A())
    vs = vecst.t
    P.dma("sync", "vec", dict(out=vs[0:32, :], in_=mixg_d.ap().rearrange("l (k p) -> (l k) p", p=128)), writes=vecst.A())
    P.dma("sync", "vec", dict(out=vs[32:64, :], in_=ffng_d.ap().rearrange("l (k p) -> (l k) p", p=128)), writes=vecst.A())
    P.dma("sync", "vec", dict(out=vs[64:72, :], in_=fing_d.ap().rearrange("(k p) -> k p", p=128)), writes=vecst.A())
    P.dma("sync", "vec", dict(out=vs[72:88, :], in_=ln_g_d.ap().rearrange("l (k p) -> (l k) p", p=128)), writes=vecst.A())
    P.dma("sync", "vec", dict(out=vs[88:104, :], in_=ln_b_d.ap().rearrange("l (k p) -> (l k) p", p=128)), writes=vecst.A())
    P.dma("sync", "vecB", dict(out=vecst2.t[0:32, :], in_=gn_g_d.ap().rearrange("l (k p) -> (l k) p", p=128)), writes=vecst2.A())
    b = nb()
    tr(ps[b][:, 0:104], vs[0:104, :], cst.t[0:104, 0:104], vecst.A() + cst.A(), PSA[b], True)
    cp("vector", gv.t[:, 0:104], ps[b][:, 0:104], PSA[b], gv.A())
    b = nb()
    tr(ps[b][:, 0:32], vecst2.t[0:32, :], cst.t[0:32, 0:32], vecst2.A() + cst.A(), PSA[b], True)
    cp("vector", gv.t[:, 104:136], ps[b][:, 0:32], PSA[b], gv.A())

    for j in range(2):
        for g in range(8):
            i = (j * 8 + g) % 2
            P.dma("sync", "wsl%d" % i, dict(out=wsl[i].t[:], in_=sgu_w_s_d.ap()[j, g]), writes=wsl[i].A())
            bsrc = bass.AP(tensor=sgu_b_s_d, offset=(j * 8 + g) * 128, ap=[[0, 128], [1, 128]])
            P.dma("sync", "bsb%d" % i, dict(out=bsb[i].t[:], in_=bsrc), writes=bsb[i].A())
            b = nb()
            tr(ps[b][:, 0:128], wsl[i].t[:], ident, wsl[i].A() + cst.A(), PSA[b], True)
            tt("vector", wtm[i].t[:], ps[b][:, 0:128], cst.t[:, C_M01:C_M01 + 128], ALU.mult,
               PSA[b] + cst.A(), wtm[i].A())
            cp("vector", wTb.t[:, j, g, :], wtm[i].t[:], wtm[i].A(), wTb.A())
            b = nb()
            mm(ps[b][:, 0:128], onesf.t[:], wtm[i].t[:], True, True, onesf.A() + wtm[i].A(), PSA[b], True)
            stt(Cc.t[:, j, g, :], ps[b][:, 0:128], gcol(G_LNB(j, g)), bsb[i].t[:], ALU.mult, ALU.add,
                PSA[b] + gv.A() + bsb[i].A(), Cc.A())

    scr_atoms = {}
    for pi, pc in enumerate(plist):
        i = pi % 2
        K, cols = pc["K"], pc["cols"]
        n = K * cols
        fv = cvf[i].t[:, 0:n].rearrange("p (k c) -> p k c", k=K)
        bv = cvb[i].t[:, 0:n].rearrange("p (k c) -> p k c", k=K)
        for (src, coff) in pc["srcs"]:
            w_ = src.shape[-1]
            P.dma("sync", "cvf%d" % i, dict(out=fv[:, :, coff:coff + w_], in_=src), writes=cvf[i].A())
        if pc["fold"] is None:
            h_ = n // 2
            cp("vector", cvb[i].t[:, 0:h_], cvf[i].t[:, 0:h_], cvf[i].A(), cvb[i].A())
            cp("scalar", cvb[i].t[:, h_:n], cvf[i].t[:, h_:n], cvf[i].A(), cvb[i].A())
        else:
            for kc in range(K):
                gc = gcol(pc["fold"][kc])
                if kc % 2 == 0:
                    ts("vector", bv[:, kc, :], fv[:, kc, :], gc, None, ALU.mult, None,
                       cvf[i].A() + gv.A(), cvb[i].A())
                else:
                    act(bv[:, kc, :], fv[:, kc, :], AF.Copy, cvf[i].A() + gv.A(), cvb[i].A(), scale=gc)
        an = "scr%d" % pi
        scr_atoms[pi] = [an]
        P.dma("gpsimd", "cvst%d" % i, dict(out=scr_d.ap()[pi][:, 0:n], in_=cvb[i].t[:, 0:n]),
              reads=cvb[i].A(), writes=[an])

    order = []
    for t_ in range(NT):
        for l in range(nlayers):
            if l % 2 == 0:
                order += [("si", l, p) for p in (2, 3, 0, 1)] + [("so", l, p) for p in range(2)]
            else:
                order += [("ri", l, p) for p in range(12)] + [("ro", l, p) for p in range(4)]
            order += [("fi", l, p) for p in range(11)] + [("fo", l, p) for p in range(8)]
    rstate = dict(next_load=0, next_use=0)

    def issue_loads(upto):
        while rstate["next_load"] < min(upto, len(order)):
            n_ = rstate["next_load"]
            pc = pieces[order[n_]]
            sl = ring[n_ % NSLOT]
            nel = pc["K"] * pc["cols"]
            P.dma("sync", "ring%d" % (n_ % NSLOT),
                  dict(out=sl.t[:, 0:nel], in_=scr_d.ap()[pc["idx"]][:, 0:nel]),
                  reads=scr_atoms[pc["idx"]], writes=sl.A())
            rstate["next_load"] += 1

    def acquire(key):
        n_ = rstate["next_use"]
        assert order[n_] == key, (order[n_], key)
        issue_loads(n_ + NSLOT)
        rstate["next_use"] += 1
        pc = pieces[key]
        sl = ring[n_ % NSLOT]
        view = sl.t[:, 0:pc["K"] * pc["cols"]].rearrange("p (k c) -> p k c", k=pc["K"])
        return view, sl.A()

    def rmsnorm_to_hn():
        for kc in range(KC):
            if kc % 2 == 1:
                act(sq.t[:, kc, :], hT.t[:, kc, :], AF.Square, hT.A(kc), sq.A(kc))
            else:
                tt("gpsimd", sq.t[:, kc, :], hT.t[:, kc, :], hT.t[:, kc, :], ALU.mult, hT.A(kc), sq.A(kc))
        act(dmy.t[:, 1:2], dmy.t[:, 0:1], AF.Ln, ["dmy.in"], ["dmy.out"])
        b = nb()
        for kc in range(KC):
            mm(ps[b][:], onesb.t[:], sq.t[:, kc, :], kc == 0, kc == KC - 1, onesb.A() + sq.A(kc), PSA[b], kc == KC - 1)
        act(ps[b][:], ps[b][:], AF.Ln, PSA[b], PSA[b], scale=1.0 / D, bias=EPS)
        act(ps[b][:], ps[b][:], AF.Exp, PSA[b], PSA[b], scale=-0.5)
        return b

    def norm_hn():
        b = rmsnorm_to_hn()
        for kc in range(KC):
            tt("vector", hn.t[:, kc, :], hT.t[:, kc, :], ps[b][:], ALU.mult, hT.A(kc) + PSA[b], hn.A(kc))

    def proj_fm(wview, watoms, ncol_chunks, col0, rhs_buf, nk, evac):
        for fc in range(ncol_chunks):
            b = nb()
            for kc in range(nk):
                mm(ps[b][:], wview[:, kc, col0 + fc * 128: col0 + (fc + 1) * 128], rhs_buf.t[:, kc, :],
                   kc == 0, kc == nk - 1, watoms + rhs_buf.A(kc), PSA[b], kc == nk - 1)
            evac(fc, b)

    def proj_tm(wview, watoms, evac):
        for c in range(NCH):
            b = nb()
            for kc in range(KC):
                mm(ps[b][:], hn.t[:, kc, c * 128:(c + 1) * 128], wview[:, kc, :],
                   kc == 0, kc == KC - 1, watoms + hn.A(kc), PSA[b], kc == KC - 1)
            evac(c, b)

    def resid_add(dm, b):
        tt("vector", hT.t[:, dm, :], ps[b][:], hT.t[:, dm, :], ALU.add, PSA[b] + hT.A(dm), hT.A(dm))

    def ffn(l, mid_hook=None):
        norm_hn()
        for p in range(11):
            if p == 3 and mid_hook is not None:
                mid_hook()
            wv, wa = acquire(("fi", l, p))
            for jj in range(2):
                j_ = 2 * p + jj
                ba = nb()
                for kc in range(KC):
                    mm(ps[ba][:], wv[:, kc, jj * 128:(jj + 1) * 128], hn.t[:, kc, :], kc == 0, kc == KC - 1,
                       wa + hn.A(kc), PSA[ba], kc == KC - 1)
                bb = nb()
                for kc in range(KC):
                    mm(ps[bb][:], wv[:, kc, 256 + jj * 128:256 + (jj + 1) * 128], hn.t[:, kc, :], kc == 0,
                       kc == KC - 1, wa + hn.A(kc), PSA[bb], kc == KC - 1)
                st_ = satmp[j_ % 2]
                act(st_.t[:], ps[ba][:], AF.Silu, PSA[ba], st_.A())
                tt("vector", hff.t[:, j_, :], st_.t[:], ps[bb][:], ALU.mult, st_.A() + PSA[bb], hff.A(j_))
        for p in range(8):
            wv, wa = acquire(("fo", l, p))
            b = nb()
            for kc in range(FFC):
                mm(ps[b][:], wv[:, kc, :], hff.t[:, kc, :], kc == 0, kc == FFC - 1, wa + hff.A(kc), PSA[b],
                   kc == FFC - 1)
            resid_add(p, b)

    def sgu(l):
        j = l // 2
        norm_hn()
        for p in range(2):
            wv, wa = acquire(("si", l, 2 + p))

            def ev(c, b, p=p):
                act(vtm.t[:, c, p * 512:(p + 1) * 512], ps[b][:], AF.Gelu_apprx_tanh, PSA[b], vtm.A(c))
            proj_tm(wv, wa, ev)
        for c in range(NCH):
            for hh in range(2):
                P.op("vector", "bn_stats", dict(out=st6.t[:, c, hh, :], in_=vtm.t[:, c, hh * 512:(hh + 1) * 512]),
                     reads=vtm.A(c), writes=st6.A(c))
            P.op("vector", "bn_aggr", dict(out=mv.t[:, c, :], in_=st6.t[:, c, :, :].rearrange("p a b -> p (a b)")),
                 reads=st6.A(c), writes=mv.A(c))
            act(lnr.t[:, c:c + 1], mv.t[:, c, 1:2], AF.Sqrt, mv.A(c), lnr.A(c), scale=1.0, bias=EPS)
            P.op("vector", "reciprocal", dict(out=lnr.t[:, c:c + 1], in_=lnr.t[:, c:c + 1]), reads=lnr.A(c), writes=lnr.A(c))
            ts("vector", vhat.t[:, c, :], vtm.t[:, c, :], mv.t[:, c, 0:1], lnr.t[:, c:c + 1], ALU.subtract, ALU.mult,
               vtm.A(c) + mv.A(c) + lnr.A(c), vhat.A(c))
        for p in range(2):
            wv, wa = acquire(("si", l, p))

            def ev(fc, b, p=p):
                f_ = p * 4 + fc
                act(uT.t[:, f_, :], ps[b][:], AF.Gelu_apprx_tanh, PSA[b], uT.A(f_))
            proj_fm(wv, wa, 4, 0, hn, KC, ev)
        for g in range(8):
            b = nb()
            for c in range(NCH):
                mm(ps[b][:, c * 128:(c + 1) * 128], vhat.t[:, c, g * 128:(g + 1) * 128], wTb.t[:, j, g, :], True, True,
                   vhat.A(c) + wTb.A(), PSA[b], c == NCH - 1)
            st_ = stmp[g % 2]
            for c in range(NCH):
                stt(st_.t[:, c * 128:(c + 1) * 128], ps[b][:, c * 128:(c + 1) * 128], gcol(G_LNG(j, g)),
                    Cc.t[:, j, g, :], ALU.mult, ALU.add, PSA[b] + gv.A() + Cc.A(), st_.A(), skip_waw=(c > 0))
            tt("gpsimd" if g % 2 else "vector", sgT.t[:, g, :], st_.t[:], uT.t[:, g, :], ALU.mult, st_.A() + uT.A(g), sgT.A(g))
        for p in range(2):
            wv, wa = acquire(("so", l, p))

            def ev(fc, b, p=p):
                resid_add(p * 4 + fc, b)
            proj_fm(wv, wa, 4, 0, sgT, KC, ev)

    gam = [1.0 - 2.0 ** (-5.0 - h) for h in range(NHEAD)]
    chunk_decay = [float(np.float32(g_ ** 128)) for g_ in gam]

    def rotary(h, b1, b2, dst, is_q, par):
        t1, t2, t3, t4 = [x_ for x_ in rtmp[par]]
        tt("vector", t1.t[:], ps[b1][:], cosT.t[:], ALU.mult, PSA[b1] + cosT.A(), t1.A())
        tt("vector", t2.t[:], ps[b2][:], sinT.t[:], ALU.mult, PSA[b2] + sinT.A(), t2.A())
        tt("vector", t3.t[:], ps[b1][:], sinT.t[:], ALU.mult, PSA[b1] + sinT.A(), t3.A())
        tt("vector", t4.t[:], ps[b2][:], cosT.t[:], ALU.mult, PSA[b2] + cosT.A(), t4.A())
        if not is_q:
            tt("gpsimd", dst.t[:, 2 * h, :], t1.t[:], t2.t[:], ALU.subtract, t1.A() + t2.A(), dst.A(2 * h))
            tt("gpsimd", dst.t[:, 2 * h + 1, :], t3.t[:], t4.t[:], ALU.add, t3.A() + t4.A(), dst.A(2 * h + 1))
        else:
            cdv = cst.t[:, C_CD + h * 128:C_CD + (h + 1) * 128]
            tt("gpsimd", t1.t[:], t1.t[:], t2.t[:], ALU.subtract, t1.A() + t2.A(), t1.A())
            tt("gpsimd", t3.t[:], t3.t[:], t4.t[:], ALU.add, t3.A() + t4.A(), t3.A())
            for c in range(NCH):
                tt("gpsimd", dst.t[:, 2 * h, c * 128:(c + 1) * 128], t1.t[:, c * 128:(c + 1) * 128], cdv, ALU.mult,
                   t1.A() + cst.A(), dst.A(2 * h), skip_waw=(c > 0))
                tt("gpsimd", dst.t[:, 2 * h + 1, c * 128:(c + 1) * 128], t3.t[:, c * 128:(c + 1) * 128], cdv, ALU.mult,
                   t3.A() + cst.A(), dst.A(2 * h + 1), skip_waw=(c > 0))

    def ret(l):
        j = l // 2
        norm_hn()
        par = [0]
        for qk in range(2):
            dst = qcd if qk == 0 else krot
            for p in range(2):
                wv, wa = acquire(("ri", l, qk * 2 + p))
                for hh in range(2):
                    h = p * 2 + hh
                    bs_ = []
                    for half in range(2):
                        b = nb()
                        cc = hh * 256 + half * 128
                        for kc in range(KC):
                            mm(ps[b][:], wv[:, kc, cc:cc + 128], hn.t[:, kc, :], kc == 0, kc == KC - 1,
                               wa + hn.A(kc), PSA[b], kc == KC - 1)
                        bs_.append(b)
                    rotary(h, bs_[0], bs_[1], dst, qk == 0, par[0])
                    par[0] ^= 1
        for p in range(4):
            wv, wa = acquire(("ri", l, 4 + p))

            def ev(c, b, p=p):
                cp(evac_eng(), vret.t[:, c, p * 512:(p + 1) * 512], ps[b][:], PSA[b], vret.A(c * 4 + p))
            proj_tm(wv, wa, ev)

        mctr = [0]

        def mb():
            b_ = 4 + (mctr[0] % 4)
            mctr[0] += 1
            return b_

        def stage_ts(c):
            bt = mb()
            for h in range(NHEAD):
                for dc in range(2):
                    idx = h * 2 + dc
                    tr(psb[bt][:, idx * 128:(idx + 1) * 128], krot.t[:, idx, c * 128:(c + 1) * 128], identb.t[:],
                       krot.A(idx) + identb.A(), PSA[bt], idx == 7)
            kd = ksd[c % 2]
            for h in range(NHEAD):
                P.op("scalar", "activation", dict(out=kd.t[:, h * 256:(h + 1) * 256], in_=psb[bt][:, h * 256:(h + 1) * 256],
                                                  func=AF.Copy, scale=cst.t[:, C_SD + h:C_SD + h + 1]),
                     reads=PSA[bt] + cst.A(), writes=kd.A(), skip_waw=(h > 0))
            bs_ = mb()
            for h in range(NHEAD):
                for dc in range(2):
                    idx = h * 2 + dc
                    mm(ps[bs_][:, h * 128:(h + 1) * 128], krot.t[:, idx, c * 128:(c + 1) * 128],
                       qcd.t[:, idx, c * 128:(c + 1) * 128], dc == 0, dc == 1, krot.A(idx) + qcd.A(idx), PSA[bs_],
                       idx == 7)
            tt("vector", sT[c % 2].t[:], ps[bs_][:], cst.t[:, C_MT:C_MT + 512], ALU.mult, PSA[bs_] + cst.A(),
               sT[c % 2].A())

        def stage_ou(c):
            kd = ksd[c % 2]
            sp = c % 2
            obanks = []
            for h in range(NHEAD):
                b = h
                for dc in range(2):
                    mm(ps[b][:], qcd.t[:, h * 2 + dc, c * 128:(c + 1) * 128], stb[j].t[:, dc, h, :], dc == 0, False,
                       qcd.A(h * 2 + dc) + stb[j].A(dc * 4 + h), PSA[b], False)
                mm(ps[b][:], sT[c % 2].t[:, h * 128:(h + 1) * 128], vret.t[:, c, h * 512:(h + 1) * 512], False, True,
                   sT[c % 2].A() + vret.A(c * 4 + h), PSA[b], True)
                obanks.append(b)
            for h in range(NHEAD):
                b = obanks[h]
                jk = junk[h % 2]
                act(jk.t[:], ps[b][:], AF.Square, PSA[b], jk.A() + ss4.A(sp), accum_out=ss4.t[:, sp, h:h + 1])
            act(rs4.t[:, sp, :], ss4.t[:, sp, :], AF.Sqrt, ss4.A(sp), rs4.A(sp), scale=1.0 / 512, bias=EPS)
            P.op("vector", "reciprocal", dict(out=rs4.t[:, sp, :], in_=rs4.t[:, sp, :]), reads=rs4.A(sp), writes=rs4.A(sp))
            for h in range(NHEAD):
                for dc in range(2):
                    b = mb()
                    mm(ps[b][:], kd.t[:, h * 256 + dc * 128:h * 256 + (dc + 1) * 128],
                       vret.t[:, c, h * 512:(h + 1) * 512], True, True, kd.A() + vret.A(c * 4 + h), PSA[b], True)
                    sa_ = stf[j].A(dc * 4 + h)
                    stt(stf[j].t[:, dc, h, :], stf[j].t[:, dc, h, :], chunk_decay[h], ps[b][:], ALU.mult, ALU.add,
                        sa_ + PSA[b], sa_)
                    cp("vector" if h == 3 else "scalar", stb[j].t[:, dc, h, :], stf[j].t[:, dc, h, :], sa_,
                       stb[j].A(dc * 4 + h))
            for h in range(NHEAD):
                b = obanks[h]
                ts("vector", on.t[:, c, h * 512:(h + 1) * 512], ps[b][:], rs4.t[:, sp, h:h + 1], None, ALU.mult, None,
                   PSA[b] + rs4.A(sp), on.A(c * 4 + h))

        stage_ts(0)
        for c in range(NCH):
            if c + 1 < NCH:
                stage_ts(c + 1)
            stage_ou(c)

        for p in range(4):
            wv, wa = acquire(("ri", l, 8 + p))

            def ev(c, b, p=p):
                sg_ = sgtmp[c % 2]
                act(sg_.t[:], ps[b][:], AF.Silu, PSA[b], sg_.A())
                tt("gpsimd" if c % 2 else "vector", on.t[:, c, p * 512:(p + 1) * 512], sg_.t[:],
                   on.t[:, c, p * 512:(p + 1) * 512], ALU.mult, sg_.A() + on.A(c * 4 + p), on.A(c * 4 + p))
            proj_tm(wv, wa, ev)
        for fcp in range(8):
            b = nb()
            for fcl in range(2):
                fc = fcp * 2 + fcl
                for c in range(NCH):
                    tr(psb[b][:, fcl * 512 + c * 128: fcl * 512 + (c + 1) * 128], on.t[:, c, fc * 128:(fc + 1) * 128],
                       identb.t[:], on.A(c * 4 + fc // 4) + identb.A(), PSA[b], fcl == 1 and c == NCH - 1)
            cp(evac_eng(), grT.t[:, fcp * 2:fcp * 2 + 2, :].rearrange("p a b -> p (a b)"), psb[b][:, 0:1024], PSA[b],
               grT.A(fcp * 2) + grT.A(fcp * 2 + 1))
        for p in range(4):
            wv, wa = acquire(("ro", l, p))

            def ev(fc, b, p=p):
                resid_add(p * 2 + fc, b)
            proj_fm(wv, wa, 2, 0, grT, 16, ev)

    def load_x(t_):
        for c in range(NCH):
            r0 = t_ * T + c * 128
            P.dma("sync", "xin%d" % c, dict(out=xin.t[:, c, :], in_=x_d.ap()[r0:r0 + 128, :]), writes=xin.A(c))

    def stage_in():
        for kc in range(KC):
            b = nb()
            for c in range(NCH):
                tr(ps[b][:, c * 128:(c + 1) * 128], xin.t[:, c, kc * 128:(kc + 1) * 128], ident,
                   xin.A(c) + cst.A(), PSA[b], c == NCH - 1)
            cp(evac_eng(), hT.t[:, kc, :], ps[b][:], PSA[b], hT.A(kc))

    def rope_tables(t_):
        psrc = bass.AP(tensor=pos_d, offset=t_ * T, ap=[[0, 128], [1, T]])
        P.dma("sync", "posi", dict(out=posi.t[:], in_=psrc), writes=posi.A())
        cp("vector", posf.t[:], posi.t[:], posi.A(), posf.A())
        ts("vector", angb.t[:], posf.t[:], cst.t[:, C_INV:C_INV + 1], None, ALU.mult, None, posf.A() + cst.A(), angb.A())
        ts("vector", kfb.t[:], angb.t[:], 1.0 / (2 * math.pi), 0.5, ALU.mult, ALU.add, angb.A(), kfb.A())
        cp("vector", kib.t[:], kfb.t[:], kfb.A(), kib.A())
        cp("vector", kfb.t[:], kib.t[:], kib.A(), kfb.A())
        c1 = 6.28125
        c2 = float(np.float32(2 * math.pi - c1))
        stt(posf.t[:], kfb.t[:], -c1, angb.t[:], ALU.mult, ALU.add, kfb.A() + angb.A(), posf.A())
        stt(posf.t[:], kfb.t[:], -c2, posf.t[:], ALU.mult, ALU.add, kfb.A() + posf.A(), posf.A())
        ts("vector", angb.t[:], posf.t[:], -math.pi, 2 * math.pi, ALU.is_lt, ALU.mult, posf.A(), angb.A())
        tt("vector", sinT.t[:], angb.t[:], posf.t[:], ALU.add, angb.A() + posf.A(), sinT.A())
        BND = 3.1415925
        ts("vector", sinT.t[:], sinT.t[:], BND, -BND, ALU.min, ALU.max, sinT.A(), sinT.A())
        act(cosT.t[:], sinT.t[:], AF.Abs, sinT.A(), cosT.A())
        act(sinT.t[:], sinT.t[:], AF.Sin, sinT.A(), sinT.A())
        act(cosT.t[:], cosT.t[:], AF.Sin, cosT.A(), cosT.A(), bias=math.pi / 2, scale=-1.0)

    out_toks = []

    def final_out(t_):
        b = rmsnorm_to_hn()
        for kc in range(KC):
            stt(fin.t[:, kc, :], hT.t[:, kc, :], gcol(G_FIN(kc)), ps[b][:], ALU.mult, ALU.mult,
                hT.A(kc) + gv.A() + PSA[b], fin.A(kc))
        for c in range(NCH):
            for half in range(2):
                b2 = nb()
                for k4 in range(4):
                    kc = half * 4 + k4
                    tr(ps[b2][:, k4 * 128:(k4 + 1) * 128], fin.t[:, kc, c * 128:(c + 1) * 128], ident,
                       fin.A(kc) + cst.A(), PSA[b2], k4 == 3)
                cp(evac_eng(), ostage.t[:, c, half * 512:(half + 1) * 512], ps[b2][:], PSA[b2], ostage.A(c))
            r0 = t_ * T + c * 128
            out_toks.append(P.dma("scalar", "out%d" % c, dict(out=out_d.ap()[r0:r0 + 128, :], in_=ostage.t[:, c, :]),
                                  reads=ostage.A(c)))

    load_x(0)
    for t_ in range(NT):
        stage_in()
        for l in range(nlayers):
            if l % 2 == 0:
                sgu(l)
            else:
                ret(l)
            if l == 0 and nlayers > 1:
                ffn(l, mid_hook=lambda t_=t_: rope_tables(t_))
            else:
                ffn(l)
            if l == nlayers - 1 and t_ + 1 < NT:
                load_x(t_ + 1)
        final_out(t_)
    P.final_wait("scalar", out_toks[-NCH:])
    P.final_wait("sync", out_toks)
    P.finish()
    return nc, P


def _host_consts():
    c = np.zeros((128, NCST), np.float64)
    c[:, C_ID:C_ID + 128] = np.eye(128)
    s = np.arange(128)[:, None]
    t = np.arange(128)[None, :]
    causal = (s <= t).astype(np.float64)
    c[:, C_M01:C_M01 + 128] = causal
    for h in range(NHEAD):
        g = 1.0 - 2.0 ** (-5.0 - h)
        c[:, C_MT + h * 128:C_MT + (h + 1) * 128] = causal * (g ** (-(s + 1.0))) / 16.0
        c[:, C_CD + h * 128:C_CD + (h + 1) * 128] = g ** (t + 1.0)
        c[:, C_SD + h] = (g ** (127.0 - np.arange(128))) / 16.0
    half = 128
    inv = 1.0 / (np.float32(10000.0) ** (np.arange(half, dtype=np.float32) / np.float32(half)))
    c[:, C_INV] = inv.astype(np.float64)
    return c.astype(np.float32)


_CACHE = {}


def kernel(x, positions, mix_norm_g, ffn_norm_g, final_norm_g,
           sgu_w_in, sgu_w_s, sgu_b_s, sgu_ln_g, sgu_ln_b, sgu_w_out,
           ret_w_in, ret_gn_g, ret_w_out, ffn_w_gu, ffn_w_down):
    if "nc" not in _CACHE:
        _CACHE["nc"] = build_program()[0]
    nc = _CACHE["nc"]
    f = lambda a: np.ascontiguousarray(np.asarray(a, dtype=np.float32))
    shared = {
        "mix_norm_g": f(mix_norm_g), "ffn_norm_g": f(ffn_norm_g), "final_norm_g": f(final_norm_g),
        "sgu_w_in": f(sgu_w_in), "sgu_w_s": f(sgu_w_s), "sgu_b_s": f(sgu_b_s), "sgu_ln_g": f(sgu_ln_g),
        "sgu_ln_b": f(sgu_ln_b), "sgu_w_out": f(sgu_w_out), "ret_w_in": f(ret_w_in), "ret_gn_g": f(ret_gn_g),
        "ret_w_out": f(ret_w_out), "ffn_w_gu": f(ffn_w_gu), "ffn_w_down": f(ffn_w_down), "cst": _host_consts(),
    }
    x = np.asarray(x)
    positions = np.asarray(positions)
    in_maps = []
    for c in range(8):
        m = dict(shared)
        m["x"] = np.ascontiguousarray(x[c], dtype=np.float32)
        m["positions"] = np.ascontiguousarray(positions[c], dtype=np.int32)
        in_maps.append(m)
    res = run_bass_kernel_spmd(nc, in_maps, core_ids=list(range(8)))
    return np.stack([np.asarray(r["out"], dtype=np.float32) for r in res.results], axis=0)
```
